# Optimizing a Trainium2 kernel written in Bass

```python
import math
import jax, jax.numpy as jnp
from jax import lax
import numpy as np

D_MODEL = 2048
BATCH = 8
SEQ = 4096
DEPTH = 1
DEC_BATCH = 1
DEC_SEQ = 16384
PAST_LEN = 128

CONV_CH = D_MODEL // 2
CONV_WIDTH = 31
CONV_PAD = CONV_WIDTH // 2
N_HEADS = 8
HEAD_DK = 64
HEAD_DV = 2 * HEAD_DK
ATTN_QK = N_HEADS * 2 * HEAD_DK
ATTN_V = N_HEADS * HEAD_DV
MIX_WIDTH = CONV_CH + ATTN_V
IN_COLS = 2 * CONV_CH + 2 * ATTN_QK + ATTN_V
D_FF = 5632
N_BUCKETS = 32
MAX_DISTANCE = 128
Q_BLOCK = 128
RMS_EPS = 1e-6
LN_EPS = 1e-5
SUBLN_EPS = 1e-5

kernel_name = "hymba_conformer_diffattn_encoder"


def rms_norm(x, g, eps=RMS_EPS):
    xf = x.astype(jnp.float32)
    y = xf * lax.rsqrt(jnp.mean(xf * xf, axis=-1, keepdims=True) + eps)
    return (y * g.astype(jnp.float32)).astype(x.dtype)


def layer_norm(x, g, b, eps=LN_EPS):
    xf = x.astype(jnp.float32)
    mu = jnp.mean(xf, axis=-1, keepdims=True)
    xc = xf - mu
    var = jnp.mean(xc * xc, axis=-1, keepdims=True)
    y = xc * lax.rsqrt(var + eps) * g.astype(jnp.float32) + b.astype(jnp.float32)
    return y.astype(x.dtype)


def swiglu(x, w_in, w_out):
    gate, up = jnp.split(x @ w_in, 2, axis=-1)
    return (jax.nn.silu(gate) * up) @ w_out


def lambda_init_fn(layer):
    return 0.8 - 0.6 * math.exp(-0.3 * layer)


def rel_bucket(rel):
    half = N_BUCKETS // 2
    max_exact = half // 2
    ret = (rel > 0).astype(jnp.int32) * half
    n = jnp.abs(rel)
    nf = jnp.maximum(n, 1).astype(jnp.float32)
    large = max_exact + (jnp.log(nf / max_exact) / math.log(MAX_DISTANCE / max_exact)
                         * (half - max_exact)).astype(jnp.int32)
    large = jnp.minimum(large, half - 1)
    return ret + jnp.where(n < max_exact, n, large)


def conv_module(a, gate, conv_w, conv_b, ln_g, ln_b):
    u = a * jax.nn.sigmoid(gate)
    y = lax.conv_general_dilated(
        u, conv_w[:, None, :].astype(u.dtype),
        window_strides=(1,), padding=[(CONV_PAD, CONV_PAD)],
        dimension_numbers=('NWC', 'WIO', 'NWC'),
        feature_group_count=CONV_CH)
    y = y + conv_b.astype(y.dtype)
    return jax.nn.silu(layer_norm(y, ln_g, ln_b))


def diff_attention(q1, q2, k1, k2, v, lam, rel_bias):
    B, H, S, _ = q1.shape
    scale = HEAD_DK ** -0.5
    kpos = jnp.arange(S, dtype=jnp.int32)
    table = rel_bias.astype(jnp.float32)

    def one_block(i):
        start = i * Q_BLOCK
        qb1 = lax.dynamic_slice_in_dim(q1, start, Q_BLOCK, axis=2)
        qb2 = lax.dynamic_slice_in_dim(q2, start, Q_BLOCK, axis=2)
        qpos = start + jnp.arange(Q_BLOCK, dtype=jnp.int32)
        bucket = rel_bucket(kpos[None, :] - qpos[:, None])
        bias = jnp.transpose(table[bucket], (2, 0, 1))
        s1 = jnp.einsum('bhqd,bhkd->bhqk', qb1, k1).astype(jnp.float32) * scale + bias
        s2 = jnp.einsum('bhqd,bhkd->bhqk', qb2, k2).astype(jnp.float32) * scale + bias
        attn = jax.nn.softmax(s1, axis=-1) - lam * jax.nn.softmax(s2, axis=-1)
        return jnp.einsum('bhqk,bhkd->bhqd', attn.astype(v.dtype), v)

    out = lax.map(one_block, jnp.arange(S // Q_BLOCK))
    return jnp.transpose(out, (1, 2, 0, 3, 4)).reshape(B, H, S, HEAD_DV)


def mixer(h_in, layer, w_in, conv_w, conv_b, conv_ln_g, conv_ln_b,
          lambda_q1, lambda_k1, lambda_q2, lambda_k2, subln_g, w_out, rel_bias):
    B, S, _ = h_in.shape
    h = h_in @ w_in
    o = 0
    a = h[..., o:o + CONV_CH]; o += CONV_CH
    g = h[..., o:o + CONV_CH]; o += CONV_CH
    q = h[..., o:o + ATTN_QK].reshape(B, S, N_HEADS, 2, HEAD_DK); o += ATTN_QK
    k = h[..., o:o + ATTN_QK].reshape(B, S, N_HEADS, 2, HEAD_DK); o += ATTN_QK
    v = h[..., o:o + ATTN_V].reshape(B, S, N_HEADS, HEAD_DV)

    conv_out = conv_module(a, g, conv_w, conv_b, conv_ln_g, conv_ln_b)

    lam_init = lambda_init_fn(layer)
    lam = (jnp.exp(jnp.sum(lambda_q1.astype(jnp.float32) * lambda_k1.astype(jnp.float32)))
           - jnp.exp(jnp.sum(lambda_q2.astype(jnp.float32) * lambda_k2.astype(jnp.float32)))
           + lam_init)
    to_bhsd = lambda t: jnp.transpose(t, (0, 2, 1, 3))
    att = diff_attention(to_bhsd(q[..., 0, :]), to_bhsd(q[..., 1, :]),
                         to_bhsd(k[..., 0, :]), to_bhsd(k[..., 1, :]),
                         to_bhsd(v), lam, rel_bias)
    att = rms_norm(att, subln_g, SUBLN_EPS) * (1.0 - lam_init)
    att = jnp.transpose(att, (0, 2, 1, 3)).reshape(B, S, ATTN_V)

    return jnp.concatenate([conv_out, att.astype(conv_out.dtype)], axis=-1) @ w_out


def run_trunk(x, rel_bias, ffn1_norm, ffn1_w_in, ffn1_w_out, mix_norm, w_in,
              conv_w, conv_b, conv_ln_g, conv_ln_b, lambda_q1, lambda_k1,
              lambda_q2, lambda_k2, subln_g, w_out, ffn2_norm, ffn2_w_in,
              ffn2_w_out, final_norm):
    for l in range(DEPTH):
        x = x + 0.5 * swiglu(rms_norm(x, ffn1_norm[l]), ffn1_w_in[l], ffn1_w_out[l])
        x = x + mixer(rms_norm(x, mix_norm[l]), l, w_in[l], conv_w[l], conv_b[l],
                      conv_ln_g[l], conv_ln_b[l], lambda_q1[l], lambda_k1[l],
                      lambda_q2[l], lambda_k2[l], subln_g[l], w_out[l], rel_bias)
        x = x + 0.5 * swiglu(rms_norm(x, ffn2_norm[l]), ffn2_w_in[l], ffn2_w_out[l])
    return rms_norm(x, final_norm)


def setup_inputs(seed: int = 0) -> dict:
    key = jax.random.key(seed)
    ks = jax.random.split(key, 24)
    f32 = jnp.float32
    nrm = lambda k, shape, s: jax.random.normal(k, shape, f32) * s
    gain = lambda k, shape: 1.0 + 0.02 * jax.random.normal(k, shape, f32)
    return {
        "x_prompt": jax.random.normal(ks[0], (BATCH, SEQ, D_MODEL), f32),
        "x_sample": jax.random.normal(ks[1], (DEC_BATCH, DEC_SEQ, D_MODEL), f32),
        "rel_bias": nrm(ks[2], (N_BUCKETS, N_HEADS), 0.5),
        "ffn1_norm": gain(ks[3], (DEPTH, D_MODEL)),
        "ffn1_w_in": nrm(ks[4], (DEPTH, D_MODEL, 2 * D_FF), D_MODEL ** -0.5),
        "ffn1_w_out": nrm(ks[5], (DEPTH, D_FF, D_MODEL), D_FF ** -0.5),
        "mix_norm": gain(ks[6], (DEPTH, D_MODEL)),
        "w_in": nrm(ks[7], (DEPTH, D_MODEL, IN_COLS), D_MODEL ** -0.5),
        "conv_w": nrm(ks[8], (DEPTH, CONV_WIDTH, CONV_CH), CONV_WIDTH ** -0.5),
        "conv_b": nrm(ks[9], (DEPTH, CONV_CH), 0.02),
        "conv_ln_g": gain(ks[10], (DEPTH, CONV_CH)),
        "conv_ln_b": nrm(ks[11], (DEPTH, CONV_CH), 0.02),
        "lambda_q1": nrm(ks[12], (DEPTH, HEAD_DK), 0.1),
        "lambda_k1": nrm(ks[13], (DEPTH, HEAD_DK), 0.1),
        "lambda_q2": nrm(ks[14], (DEPTH, HEAD_DK), 0.1),
        "lambda_k2": nrm(ks[15], (DEPTH, HEAD_DK), 0.1),
        "subln_g": gain(ks[16], (DEPTH, HEAD_DV)),
        "w_out": nrm(ks[17], (DEPTH, MIX_WIDTH, D_MODEL), MIX_WIDTH ** -0.5),
        "ffn2_norm": gain(ks[18], (DEPTH, D_MODEL)),
        "ffn2_w_in": nrm(ks[19], (DEPTH, D_MODEL, 2 * D_FF), D_MODEL ** -0.5),
        "ffn2_w_out": nrm(ks[20], (DEPTH, D_FF, D_MODEL), D_FF ** -0.5),
        "final_norm": gain(ks[21], (D_MODEL,)),
    }


def reference(x_prompt, x_sample, rel_bias, ffn1_norm, ffn1_w_in, ffn1_w_out, mix_norm,
              w_in, conv_w, conv_b, conv_ln_g, conv_ln_b, lambda_q1, lambda_k1,
              lambda_q2, lambda_k2, subln_g, w_out, ffn2_norm, ffn2_w_in, ffn2_w_out,
              final_norm):
    y_prompt = run_trunk(x_prompt, rel_bias, ffn1_norm, ffn1_w_in, ffn1_w_out, mix_norm,
                         w_in, conv_w, conv_b, conv_ln_g, conv_ln_b, lambda_q1, lambda_k1,
                         lambda_q2, lambda_k2, subln_g, w_out, ffn2_norm, ffn2_w_in,
                         ffn2_w_out, final_norm)
    y_sample = run_trunk(x_sample, rel_bias, ffn1_norm, ffn1_w_in, ffn1_w_out, mix_norm,
                         w_in, conv_w, conv_b, conv_ln_g, conv_ln_b, lambda_q1, lambda_k1,
                         lambda_q2, lambda_k2, subln_g, w_out, ffn2_norm, ffn2_w_in,
                         ffn2_w_out, final_norm)
    return (y_prompt, y_sample)
```

```python
import contextlib
import math
import numpy as np
import concourse.bass as bass
import concourse.mybir as mybir
from concourse.bass_utils import run_bass_kernel_spmd

F32, BF16 = mybir.dt.float32, mybir.dt.bfloat16
AF = mybir.ActivationFunctionType
ALU = mybir.AluOpType
NCORES = 8
D = 2048
KC = D // 128
CONV = 1024
NH = 8
CW = 31
PADC = 16
RMS_EPS = 1e-6
LN_EPS = 1e-5
SUBLN_EPS = 1e-5
LAM_INIT = 0.8 - 0.6 * math.exp(-0.3 * 0)
FLEN = 1280
ENG = ('sp', 'pe', 'act', 'dve', 'pool')


class Sem:
    __slots__ = ("h", "v")


class Rec:
    def __init__(self, nc, es):
        self.nc, self.es = nc, es
        self.q = {k: [] for k in ENG}
        self.waited = {}
        self.sems = []
        self.prog = {k: self.sem("p_" + k) for k in ('pe', 'act', 'dve', 'pool')}

    def sem(self, name):
        s = Sem()
        s.h = self.es.enter_context(self.nc.semaphore(name))
        s.v = 0
        self.sems.append(s)
        return s

    def _waits(self, eng, waits):
        for t in waits:
            if t is None:
                continue
            s, v = t
            if v <= 0:
                continue
            key = (eng, id(s))
            if self.waited.get(key, 0) >= v:
                continue
            self.waited[key] = v
            self.q[eng].append(lambda e, s=s, v=v: e.wait_ge(s.h, v))

    def op(self, eng, fn, waits=(), sig=True):
        self._waits(eng, waits)
        if sig:
            s = self.prog[eng]
            s.v += 1
            v = s.v
            self.q[eng].append(lambda e, fn=fn, s=s: fn(e).then_inc(s.h, 1))
            return (s, v)
        self.q[eng].append(fn)
        return None

    def dma(self, eng, out, in_, sem, waits=(), slow=False):
        self._waits(eng, waits)
        sem.v += 16
        v = sem.v
        if slow:
            self.q[eng].append(lambda e, o=out, i=in_, s=sem: e.dma_start(out=o, in_=i, allow_slow_non_contiguous=True).then_inc(s.h, 16))
        else:
            self.q[eng].append(lambda e, o=out, i=in_, s=sem: e.dma_start(out=o, in_=i).then_inc(s.h, 16))
        return (sem, v)

    def cc(self, kind, op, in_ap, out_ap, waits=()):
        self._waits('pool', waits)
        s = self.sem("cc%d" % len(self.sems))
        s.v = 1
        self.q['pool'].append(lambda e, s=s: e.collective_compute(
            kind, op, replica_groups=[list(range(NCORES))], ins=[in_ap], outs=[out_ap]).then_inc(s.h, 1))
        return (s, 1)

    def barrier(self, exclude=None):
        tickets = [(s, s.v) for s in self.sems if s.v > 0 and s is not exclude]
        for eng in ENG:
            self._waits(eng, tickets)


class Stream:
    def __init__(self, R, name, nslots, n, make, eng='sp', first_waits=()):
        self.R, self.ns, self.n, self.make, self.eng = R, nslots, n, make, eng
        self.sems = [R.sem("%s%d" % (name, i)) for i in range(nslots)]
        self.ready = {}
        self.first_waits = list(first_waits)
        for i in range(min(nslots, n)):
            self._issue(i, self.first_waits)

    def _issue(self, i, waits):
        slot = i % self.ns
        t = None
        for (o, a) in self.make(i, slot):
            t = self.R.dma(self.eng, o, a, self.sems[slot], waits=waits)
            waits = ()
        self.ready[i] = t

    def get(self, i):
        return i % self.ns, self.ready[i]

    def release(self, i, tickets):
        j = i + self.ns
        if j < self.n:
            self._issue(j, list(tickets) + self.first_waits)


def rel_bucket_np(rel):
    half, max_exact = 16, 8
    ret = (rel > 0).astype(np.int32) * half
    n = np.abs(rel)
    nf = np.maximum(n, 1).astype(np.float32)
    large = max_exact + (np.log(nf / np.float32(max_exact)) / np.float32(math.log(128 / max_exact))
                         * np.float32(half - max_exact)).astype(np.int32)
    large = np.minimum(large, half - 1)
    return ret + np.where(n < max_exact, n, large)


def build_nc(SP, SS, DFF, do_mixer=True):
    SL = SS // NCORES
    NT = SP + SL
    HC = DFF // 128
    NSLAB1 = HC // 2
    KG = 4
    KPG = HC // KG
    assert HC % 2 == 0 and HC % KG == 0 and SP % 512 == 0 and SL % 512 == 0
    nc = bass.Bass("TRN2", target_bir_lowering=False)

    def din(name, shape, dt=F32):
        return nc.dram_tensor(name, list(shape), dt, kind="ExternalInput").ap()

    def dscr(name, shape, dt):
        return nc.dram_tensor(name, list(shape), dt)

    xs = din("xs", [NT, D])
    w_src = {
        "w1i": din("w1i", [256, 2 * DFF]), "w1o": din("w1o", [DFF // 8, D]),
        "wi": din("wi", [256, 5120]), "wo": din("wo", [256, D]),
        "w2i": din("w2i", [256, 2 * DFF]), "w2o": din("w2o", [DFF // 8, D]),
        "wis": din("wis", [D, 640]), "wos": din("wos", [256, D]),
    }
    g_src = {"w1i": din("g1", [256, 1]), "wi": din("gm", [256, 1]), "w2i": din("g2", [256, 1]),
             "wis": din("gmf", [D, 1])}
    gfin = din("gfin", [1, D])
    convw = din("convw", [CONV, CW]); convb = din("convb", [CONV, 1])
    lng = din("lng", [CONV, 1]); lnb = din("lnb", [CONV, 1])
    convw_s = din("convw_s", [128, CW]); convb_s = din("convb_s", [128, 1])
    lng_s = din("lng_s", [128, 1]); lnb_s = din("lnb_s", [128, 1])
    lam4 = din("lam4", [64, 4]); subg = din("subg", [128, 1])
    relb = din("relb", [32, NH]); relb_s = din("relb_s", [32, 1])
    ohrev = din("ohrev", [32, FLEN]); identf_d = din("identf", [128, 128]); jflip_d = din("jflip", [128, 128])
    y = nc.dram_tensor("y", [NT, D], F32, kind="ExternalOutput").ap()

    wshape = {"w1i": (D, 2 * DFF), "w1o": (DFF, D), "wi": (D, 5120), "wo": (D, D),
              "w2i": (D, 2 * DFF), "w2o": (DFF, D)}
    wloc = {k: dscr("wl_" + k, [v[0] // 8, v[1]], BF16) for k, v in wshape.items()}
    wfull = {k: dscr("wf_" + k, list(v), BF16) for k, v in wshape.items()}
    wis_b = dscr("wis_b", [D, 640], BF16)
    wos_b = dscr("wos_b", [256, D], BF16)
    x1d = dscr("x1d", [NT, D], F32)
    xn2l = dscr("xn2l", [D, SL], BF16)
    xn2a = dscr("xn2a", [NCORES * D, SL], BF16)
    grp = {}
    for g, (S, C, H) in {"p": (SP, 8, 8), "s": (SS, 1, 1)}.items():
        grp[g] = dict(
            S=S, C=C, H=H,
            uT=dscr("uT_" + g, [C * 128, S + 2 * PADC], F32),
            yT=dscr("yT_" + g, [C * 128, S], F32),
            st=dscr("st_" + g, [2, S], F32),
            qT=dscr("qT_" + g, [H * 128, S], BF16),
            kT=dscr("kT_" + g, [H * 128, S], BF16),
            V=dscr("V_" + g, [H, S, 128], BF16),
            mixT=dscr("mixT_" + g, [(C + H) * 128, S], BF16),
            F=dscr("F_" + g, [H, FLEN], F32),
        )
    st_all = dscr("st_all", [2, SS], F32)
    part = dscr("part", [SS, D], F32)
    rso = dscr("rso", [SL, D], F32)

    es = contextlib.ExitStack()
    with es:
        R = Rec(nc, es)

        def sb(name, shape, dt):
            return es.enter_context(nc.sbuf_tensor("sb_" + name, list(shape), dt))

        identf = sb("identf", [128, 128], F32)
        identb = sb("identb", [128, 128], BF16)
        jflip = sb("jflip", [128, 128], F32)
        onesf = sb("onesf", [128, 128], F32)
        onesb = sb("onesb", [128, 128], BF16)
        gfin_bc = sb("gfin_bc", [128, D], F32)
        cpar = sb("cpar", [128, 9, CW + 3], F32)
        gcols = sb("gcols", [128, 32], F32)
        small = sb("small", [128, 64], F32)
        relb_sb = sb("relb_sb", [32, NH + 1], F32)
        oh_sb = sb("oh_sb", [32, FLEN], F32)
        subg_sb = sb("subg_sb", [128, 1], F32)
        lam_sb = sb("lam_sb", [64, 4], F32)
        arena_f = sb("arena_f", [128, 11264], F32)
        arena_b = sb("arena_b", [128, 69120], BF16)
        psum = es.enter_context(nc.psum_tensor("psum", [128, 4096], F32))

        def bank(i, n=1):
            return psum[:, i * 512:(i + n) * 512]

        bank_free = [[] for _ in range(8)]
        bank_rr = [0]

        def alloc_banks(n, lo=0, hi=8):
            cnt = (hi - lo) // n
            i = lo + (bank_rr[0] % cnt) * n
            bank_rr[0] += 1
            w = []
            for b in range(i, i + n):
                w += bank_free[b]
                bank_free[b] = []
            return i, w

        def free_banks(i, n, tickets):
            for b in range(i, i + n):
                bank_free[b] = list(tickets)

        csem = R.sem("csem")
        tl = []
        tl.append(R.dma('sp', identf[:, :], identf_d, csem))
        tl.append(R.dma('sp', jflip[:, :], jflip_d, csem))
        tl.append(R.dma('sp', gfin_bc[:, :], bass.AP(gfin.tensor, 0, [[0, 128], [1, D]]), csem))
        tl.append(R.dma('sp', cpar[:, 0:8, 0:CW], convw.rearrange("(c p) w -> p c w", p=128), csem))
        tl.append(R.dma('sp', cpar[:, 0:8, CW:CW + 1], convb.rearrange("(c p) w -> p c w", p=128), csem, slow=True))
        tl.append(R.dma('sp', cpar[:, 0:8, CW + 1:CW + 2], lng.rearrange("(c p) w -> p c w", p=128), csem, slow=True))
        tl.append(R.dma('sp', cpar[:, 0:8, CW + 2:CW + 3], lnb.rearrange("(c p) w -> p c w", p=128), csem, slow=True))
        tl.append(R.dma('sp', cpar[:, 8, 0:CW], convw_s, csem))
        tl.append(R.dma('sp', cpar[:, 8, CW:CW + 1], convb_s, csem))
        tl.append(R.dma('sp', cpar[:, 8, CW + 1:CW + 2], lng_s, csem))
        tl.append(R.dma('sp', cpar[:, 8, CW + 2:CW + 3], lnb_s, csem))
        tl.append(R.dma('sp', relb_sb[:, 0:NH], relb, csem))
        tl.append(R.dma('sp', relb_sb[:, NH:NH + 1], relb_s, csem))
        tl.append(R.dma('sp', oh_sb[:, :], ohrev, csem))
        tl.append(R.dma('sp', subg_sb[:, :], subg, csem))
        tl.append(R.dma('sp', lam_sb[:, :], lam4, csem))
        t_const = tl[-1]
        R.op('dve', lambda e: e.tensor_copy(out=identb[:, :], in_=identf[:, :]), waits=[t_const])
        R.op('dve', lambda e: e.memset(onesf[:, :], 1.0))
        R.op('dve', lambda e: e.memset(onesb[:, :], 1.0))
        zpad = sb("zpad", [128, PADC], F32)
        R.op('dve', lambda e: e.memset(zpad[:, :], 0.0))
        t_z = R.op('dve', lambda e: e.memset(small[:, 0:32], 0.0))
        R.op('dve', lambda e: e.memset(small[:, 48:49], RMS_EPS))
        R.op('dve', lambda e: e.memset(small[:, 49:50], LN_EPS))
        epsr, epsl = small[:, 48:49], small[:, 49:50]
        zsem = R.sem("zsem")
        for g in ("p", "s"):
            G = grp[g]
            for c in range(G["C"]):
                for off in (0, PADC + G["S"]):
                    R.dma('pool', G["uT"][c * 128:(c + 1) * 128, off:off + PADC], zpad[:, :], zsem, waits=[t_z])

        if do_mixer:
            t = R.op('dve', lambda e: e.tensor_tensor(out=lam_sb[:, 0:1], in0=lam_sb[:, 0:1], in1=lam_sb[:, 1:2], op=ALU.mult), waits=[t_const])
            t = R.op('dve', lambda e: e.tensor_tensor(out=lam_sb[:, 1:2], in0=lam_sb[:, 2:3], in1=lam_sb[:, 3:4], op=ALU.mult), waits=[t])
            bi, w = alloc_banks(1)
            t = R.op('pe', lambda e, bi=bi: e.matmul(bank(bi)[:, 0:2], lhsT=onesf[0:64, :], rhs=lam_sb[:, 0:2], start=True, stop=True), waits=[t] + w)
            t = R.op('act', lambda e, bi=bi: e.activation(out=small[:, 40:42], in_=bank(bi)[:, 0:2], func=AF.Exp), waits=[t])
            free_banks(bi, 1, [t])
            t = R.op('dve', lambda e: e.tensor_tensor(out=small[:, 42:43], in0=small[:, 41:42], in1=small[:, 40:41], op=ALU.subtract), waits=[t])
            t_lam = R.op('dve', lambda e: e.tensor_scalar(out=small[:, 43:44], in0=small[:, 42:43], scalar1=-LAM_INIT, scalar2=None, op0=ALU.add), waits=[t])
        neg_lam = small[:, 43:44]

        pin = [arena_f[:, 0:2048], arena_f[:, 2048:4096]]
        pout = [arena_b[:, 0:2048], arena_b[:, 2048:4096]]
        sin_ = [R.sem("pin0"), R.sem("pin1")]
        sout = [R.sem("pout0"), R.sem("pout1")]
        gsem = R.sem("gsem")
        free_in = [None, None]
        free_out = [None, None]
        it = [0]
        gcol_i = [0]
        w_ready = {}

        gmap = {}
        t_g = None
        for key_, rows_ in (("w1i", 256), ("wi", 256), ("w2i", 256), ("wis", D)):
            for r0_ in range(0, rows_, 128):
                gmap[(key_, r0_)] = len(gmap)
                t_g = R.dma('sp', gcols[:, gmap[(key_, r0_)]:gmap[(key_, r0_)] + 1], g_src[key_][r0_:r0_ + 128, :], gsem)

        def prepass(key, dst, rows):
            src = w_src[key]
            C = src.shape[1]
            outs = []
            for r0 in range(0, rows, 128):
                pr = min(128, rows - r0)
                gt = None
                gc = None
                if key in g_src:
                    gc = gcols[0:pr, gmap[(key, r0)]:gmap[(key, r0)] + 1]
                    gt = t_g
                for c0 in range(0, C, 2048):
                    cw = min(2048, C - c0)
                    s = it[0] % 2
                    it[0] += 1
                    t_in = R.dma('sp', pin[s][0:pr, 0:cw], src[r0:r0 + pr, c0:c0 + cw], sin_[s], waits=[free_in[s]])
                    eng = 'dve' if it[0] % 3 else 'pool'
                    if gc is not None:
                        fn = lambda e, s=s, pr=pr, cw=cw, gc=gc: e.tensor_scalar(out=pout[s][0:pr, 0:cw], in0=pin[s][0:pr, 0:cw], scalar1=gc, scalar2=None, op0=ALU.mult)
                    else:
                        fn = lambda e, s=s, pr=pr, cw=cw: e.tensor_copy(out=pout[s][0:pr, 0:cw], in_=pin[s][0:pr, 0:cw])
                    t_op = R.op(eng, fn, waits=[t_in, gt, free_out[s]])
                    free_in[s] = t_op
                    t_out = R.dma('pool', dst[r0:r0 + pr, c0:c0 + cw], pout[s][0:pr, 0:cw], sout[s], waits=[t_op])
                    free_out[s] = t_out
                    outs.append(t_out)
            return outs

        for key in ("w1i", "w1o", "wi", "wo", "w2i", "w2o"):
            outs = prepass(key, wloc[key], wshape[key][0] // 8)
            w_ready[key] = R.cc("AllGather", ALU.bypass, wloc[key].ap().opt(), wfull[key].ap().opt(), waits=outs)
        w_ready["wis"] = prepass("wis", wis_b, D)
        w_ready["wos"] = prepass("wos", wos_b, 256)
        R.barrier()

        xt = arena_f[:, 0:8192].rearrange("p (b d) -> p b d", d=D)
        sg = [arena_f[:, 8192:8704], arena_f[:, 8704:9216]]
        ostage = [arena_f[:, 9216:9728], arena_f[:, 9728:10240]]
        ss = small[:, 0:32]
        xnb = [arena_b[:, i * 2048:(i + 1) * 2048] for i in range(4)]
        xnT = arena_b[:, 8192:16384].rearrange("p (k t) -> p k t", t=512)
        hT = arena_b[:, 16384:16384 + HC * 512].rearrange("p (k t) -> p k t", t=512)
        o0 = 16384 + 44 * 512
        WA = [arena_b[:, o0 + i * 8192:o0 + (i + 1) * 8192].rearrange("p (k n) -> p k n", n=512) for i in range(2)]
        o1 = o0 + 2 * 8192
        W2 = [arena_b[:, o1 + i * 5632:o1 + i * 5632 + KPG * 512].rearrange("p (k n) -> p k n", n=512) for i in range(2)]
        o2 = o1 + 2 * 5632
        bstage = [arena_b[:, o2 + i * 512:o2 + (i + 1) * 512] for i in range(4)]
        o3 = o2 + 4 * 512
        vT_sb = arena_b[:, o3:o3 + 512]
        assert o3 + 512 <= 69120

        st_sem = {k: R.sem("st_" + k) for k in ("o0", "o1", "b0", "b1", "b2", "b3", "x0", "x1", "x2", "x3", "xn")}
        ost_free = [None, None]
        bst_free = [None, None, None, None]
        ost_i = [0]
        bst_i = [0]
        xt_ready = [None] * 4
        xt_free = [[] for _ in range(4)]
        xsem = [R.sem("x%d" % b) for b in range(4)]
        ss_i = [0]
        xnb_free = [None, None, None, None]
        xnb_i = [0]
        xnT_free = [[]]
        hT_free = [[]]
        sg_free = [None, None]
        sg_i = [0]

        def wview(key):
            return wfull[key].ap().rearrange("(k p) n -> p k n", p=128)

        def rmsnorm_tile(src_ready, after=None):
            base = (ss_i[0] % 4) * 4
            ss_i[0] += 1
            t = None
            for b in range(4):
                t = R.op('act', lambda e, b=b: e.activation(out=xnb[b], in_=xt[:, b, :], func=AF.Square, accum_out=ss[:, base + b:base + b + 1]),
                         waits=[src_ready[b], xnb_free[b]])
            t = R.op('act', lambda e: e.activation(out=ss[:, 16 + base:20 + base], in_=ss[:, base:base + 4], func=AF.Sqrt, bias=epsr, scale=1.0 / D), waits=[t])
            t_r = R.op('dve', lambda e: e.reciprocal(out=ss[:, 16 + base:20 + base], in_=ss[:, 16 + base:20 + base]), waits=[t])
            t_s = []
            for b in range(4):
                if b % 2 == 0:
                    t_s.append(R.op('act', lambda e, b=b: e.activation(out=xnb[b], in_=xt[:, b, :], func=AF.Copy, scale=ss[:, 16 + base + b:17 + base + b]), waits=[t_r]))
                else:
                    t_s.append(R.op('dve', lambda e, b=b: e.tensor_scalar(out=xnb[b], in0=xt[:, b, :], scalar1=ss[:, 16 + base + b:17 + base + b], scalar2=None, op0=ALU.mult), waits=[t_r]))
            outs = []
            tps = []
            for b in range(4):
                bi, w = alloc_banks(2)
                pv = bank(bi, 2).bitcast(BF16)
                tp = None
                for k in range(KC):
                    last = (k == KC - 1)
                    r = R.op('pe', lambda e, k=k, b=b, pv=pv: e.transpose(pv[:, k * 128:(k + 1) * 128], xnb[b][:, k * 128:(k + 1) * 128], identb[:, :]),
                             waits=([t_s[b]] + w + xnT_free[0]) if k == 0 else (), sig=last)
                    if last:
                        tp = r
                xnb_free[b] = tp
                tps.append((tp, bi, pv))
            for b in range(4):
                tp, bi, pv = tps[b]
                t1 = R.op('act', lambda e, b=b, pv=pv: e.activation(out=xnT[:, 0:8, b * 128:(b + 1) * 128], in_=pv[:, 0:1024].rearrange("p (k t) -> p k t", t=128), func=AF.Copy), waits=[tp])
                t2 = R.op('dve', lambda e, b=b, pv=pv: e.tensor_copy(out=xnT[:, 8:16, b * 128:(b + 1) * 128], in_=pv[:, 1024:2048].rearrange("p (k t) -> p k t", t=128)), waits=[tp])
                free_banks(bi, 2, [t1, t2])
                outs += [t1, t2]
            xnT_free[0] = []
            return outs

        def ffn(xn_ready, wa, wa_base, w2, w2_base, tile_i):
            h_done = []
            first = True
            for j in range(NSLAB1):
                idx = wa_base + j
                slot, rdy = wa.get(idx)
                last_pe = None
                for cc_ in range(2):
                    c = 2 * j + cc_
                    bi, w = alloc_banks(2)
                    for half in range(2):
                        for k in range(KC):
                            lastk = (k == KC - 1)
                            wt = []
                            if k == 0 and half == 0:
                                wt = [rdy] + w + (xn_ready + hT_free[0] if first else [])
                                first = False
                            r = R.op('pe', lambda e, slot=slot, k=k, half=half, cc_=cc_, bi=bi, lastk=lastk: e.matmul(
                                bank(bi + half), lhsT=WA[slot][:, k, half * 256 + cc_ * 128:half * 256 + cc_ * 128 + 128],
                                rhs=xnT[:, k, :], start=(k == 0), stop=lastk), waits=wt, sig=(lastk and half == 1))
                            if lastk and half == 1:
                                last_pe = r
                    s = sg_i[0] % 2
                    sg_i[0] += 1
                    ta = R.op('act', lambda e, s=s, bi=bi: e.activation(out=sg[s], in_=bank(bi), func=AF.Silu), waits=[last_pe, sg_free[s]])
                    td = R.op('dve', lambda e, s=s, bi=bi, c=c: e.tensor_tensor(out=hT[:, c, :], in0=bank(bi + 1), in1=sg[s], op=ALU.mult), waits=[ta, last_pe])
                    sg_free[s] = td
                    free_banks(bi, 2, [td])
                    h_done.append(td)
                wa.release(idx, [last_pe])
            hT_free[0] = []
            xnT_free[0] = [last_pe]
            out_t = [None] * 4
            lastmm = None
            for s4 in range(4):
                bis = []
                ws = []
                for b in range(4):
                    bi, w = alloc_banks(1)
                    bis.append(bi)
                    ws += w
                for kg in range(KG):
                    idx = w2_base + s4 * KG + kg
                    slot, rdy = w2.get(idx)
                    for kk in range(KPG):
                        k = kg * KPG + kk
                        for b in range(4):
                            wt = []
                            if kk == 0 and b == 0:
                                wt = [rdy] + (ws + h_done[-1:] if kg == 0 else [])
                            lastk = (k == HC - 1)
                            r = R.op('pe', lambda e, slot=slot, kk=kk, k=k, b=b, bi=bis[b], lastk=lastk: e.matmul(
                                bank(bi), lhsT=hT[:, k, b * 128:(b + 1) * 128], rhs=W2[slot][:, kk, :], start=(k == 0), stop=lastk),
                                waits=wt, sig=(lastk or (kk == KPG - 1 and b == 3)))
                            if lastk or (kk == KPG - 1 and b == 3):
                                lastmm = r
                            if lastk:
                                tb = r
                                td = R.op('dve', lambda e, b=b, s4=s4, bi=bis[b]: e.scalar_tensor_tensor(
                                    out=xt[:, b, s4 * 512:(s4 + 1) * 512], in0=bank(bi), scalar=0.5, in1=xt[:, b, s4 * 512:(s4 + 1) * 512],
                                    op0=ALU.mult, op1=ALU.add), waits=[tb])
                                free_banks(bis[b], 1, [td])
                                out_t[b] = td
                    w2.release(idx, [lastmm])
            hT_free[0] = [lastmm]
            return out_t

        tiles = [("s", i) for i in range(SL // 512)] + [("p", i) for i in range(SP // 512)]
        NTILE = len(tiles)

        def tok0(tl_):
            g, i = tl_
            return (SP + i * 512) if g == "s" else i * 512

        itemsA = []
        for ti, tl_ in enumerate(tiles):
            for j in range(NSLAB1):
                itemsA.append(("w1i", j))
            if tl_[0] == "p" and do_mixer:
                for j in range(10):
                    itemsA.append(("wi", j))

        def mk_slab(items):
            def make(i, slot):
                key, j = items[i]
                wv = wview(key)
                if key in ("w1i", "w2i"):
                    return [(WA[slot][:, :, 0:256], wv[:, :, j * 256:(j + 1) * 256]),
                            (WA[slot][:, :, 256:512], wv[:, :, DFF + j * 256:DFF + (j + 1) * 256])]
                if key == "wi":
                    if j < 4:
                        return [(WA[slot][:, :, 0:256], wv[:, :, j * 256:(j + 1) * 256]),
                                (WA[slot][:, :, 256:512], wv[:, :, 1024 + j * 256:1024 + (j + 1) * 256])]
                    return [(WA[slot][:, :, :], wv[:, :, 2048 + (j - 4) * 512:2048 + (j - 3) * 512])]
                return [(WA[slot][:, :, :], wv[:, :, j * 512:(j + 1) * 512])]
            return make

        def mk_w2(key):
            def make(i, slot):
                r = i % (4 * KG)
                s4, kg = r // KG, r % KG
                return [(W2[slot][:, :, :], wview(key)[:, kg * KPG:(kg + 1) * KPG, s4 * 512:(s4 + 1) * 512])]
            return make

        wts = [w_ready[k] for k in ("w1i", "w1o", "wi", "wo", "w2i", "w2o")]
        stA = Stream(R, "wa", 2, len(itemsA), mk_slab(itemsA), first_waits=wts)
        stW2 = Stream(R, "w2a", 2, NTILE * 4 * KG, mk_w2("w1o"), first_waits=wts)

        def store_f32(dst_ap, src_fn_eng, src_fn, waits):
            s = ost_i[0] % 2
            ost_i[0] += 1
            t = R.op(src_fn_eng, lambda e, s=s: src_fn(e, ostage[s]), waits=list(waits) + [ost_free[s]])
            td = R.dma('pool', dst_ap, ostage[s], st_sem["o%d" % s], waits=[t])
            ost_free[s] = td
            return t, td

        def store_bf16(dst_ap, eng, src_fn, waits, width=512, view=None):
            s = bst_i[0] % 4
            bst_i[0] += 1
            t = R.op(eng, lambda e, s=s: src_fn(e, bstage[s][:, 0:width]), waits=list(waits) + [bst_free[s]])
            sv = bstage[s][:, 0:width]
            if view is not None:
                sv = view(sv)
            td = R.dma('pool', dst_ap, sv, st_sem["b%d" % s], waits=[t])
            bst_free[s] = td
            return t, td

        vT_free = [None]

        def epi_u(G, c, c0t, ba, ra, bg, rg):
            s = sg_i[0] % 2
            sg_i[0] += 1
            ta = R.op('act', lambda e: e.activation(out=sg[s], in_=bank(bg), func=AF.Sigmoid), waits=[rg, sg_free[s]])
            t, td = store_f32(G["uT"][c * 128:(c + 1) * 128, PADC + c0t:PADC + c0t + 512], 'dve',
                              lambda e, o: e.tensor_tensor(out=o, in0=bank(ba), in1=sg[s], op=ALU.mult), [ta, ra])
            sg_free[s] = t
            free_banks(ba, 1, [t])
            free_banks(bg, 1, [t])

        def epi_q(G, h, c0t, bi, r):
            t, _ = store_bf16(G["qT"][h * 128:(h + 1) * 128, c0t:c0t + 512], 'act',
                              lambda e, o: e.activation(out=o, in_=bank(bi), func=AF.Copy, scale=0.125), [r])
            free_banks(bi, 1, [t])

        def epi_k(G, h, c0t, bi, r):
            t, _ = store_bf16(G["kT"][h * 128:(h + 1) * 128, c0t:c0t + 512], 'dve',
                              lambda e, o: e.tensor_copy(out=o, in_=bank(bi)), [r])
            free_banks(bi, 1, [t])

        def epi_v(G, h, c0t, bi, r):
            t1 = R.op('act', lambda e: e.activation(out=vT_sb, in_=bank(bi), func=AF.Copy), waits=[r, vT_free[0]])
            free_banks(bi, 1, [t1])
            b2, w = alloc_banks(1)
            pv = bank(b2).bitcast(BF16)
            tp = None
            for b in range(4):
                tp = R.op('pe', lambda e, b=b: e.transpose(pv[:, b * 128:(b + 1) * 128], vT_sb[:, b * 128:(b + 1) * 128], identb[:, :]),
                          waits=([t1] + w) if b == 0 else (), sig=(b == 3))
            vT_free[0] = tp
            t, _ = store_bf16(G["V"][h, c0t:c0t + 512, :].rearrange("(b p) d -> p b d", p=128), 'dve',
                              lambda e, o: e.tensor_copy(out=o, in_=pv[:, 0:512]), [tp],
                              view=lambda a: a.rearrange("p (b d) -> p b d", d=128))
            free_banks(b2, 1, [t])

        def project(G, c0t, chunks, lhs_fn, first_waits):
            pend_a = {}
            lastpe = None
            fw = list(first_waits)
            for n, (kind, idx, rdy) in enumerate(chunks):
                bi, w = alloc_banks(1)
                r = None
                xv = cur_xnT[0]
                for k in range(KC):
                    lastk = (k == KC - 1)
                    wt = ([rdy] + w + fw) if k == 0 else []
                    if k == 0:
                        fw = []
                    r = R.op('pe', lambda e, n=n, k=k, bi=bi, lastk=lastk, xv=xv, la=lhs_fn(n, k): e.matmul(
                        bank(bi), lhsT=la, rhs=xv[:, k, :], start=(k == 0), stop=lastk), waits=wt, sig=lastk)
                lastpe = r
                if kind == "a":
                    pend_a[idx] = (bi, r)
                elif kind == "g":
                    ba, ra = pend_a.pop(idx)
                    epi_u(G, idx, c0t, ba, ra, bi, r)
                elif kind == "q":
                    epi_q(G, idx, c0t, bi, r)
                elif kind == "k":
                    epi_k(G, idx, c0t, bi, r)
                else:
                    epi_v(G, idx, c0t, bi, r)
            return lastpe

        cur_xnT = [xnT]

        posA = 0
        pG, sG = grp["p"], grp["s"]
        xn2_store = []
        for ti, tl_ in enumerate(tiles):
            t0 = tok0(tl_)
            for b in range(4):
                xt_ready[b] = R.dma('sp', xt[:, b, :], xs[t0 + b * 128:t0 + (b + 1) * 128, :], xsem[b], waits=xt_free[b])
                xt_free[b] = []
            xr = rmsnorm_tile(xt_ready)
            x1t = ffn(xr, stA, posA, stW2, ti * 4 * KG, ti)
            posA += NSLAB1
            for b in range(4):
                td = R.dma('pool', x1d[t0 + b * 128:t0 + (b + 1) * 128, :], xt[:, b, :], st_sem["x%d" % b], waits=[x1t[b]])
                xt_free[b] = [td]
            if not do_mixer:
                continue
            xr2 = rmsnorm_tile(x1t)
            for b in range(4):
                xt_free[b] = xt_free[b] + [xr2[2 * b], xr2[2 * b + 1]]
            if tl_[0] == "s":
                i = tl_[1]
                td = R.dma('pool', xn2l.ap().rearrange("(k p) t -> p k t", p=128)[:, :, i * 512:(i + 1) * 512], xnT[:, :, :], st_sem["xn"], waits=xr2)
                xn2_store.append(td)
                xnT_free[0] = [td]
                continue
            i = tl_[1]
            c0t = i * 512
            last = None
            fw = xr2
            for j in range(10):
                idx = posA + j
                slot, rdy = stA.get(idx)
                if j < 4:
                    chunks = [("a", 2 * j, rdy), ("a", 2 * j + 1, rdy), ("g", 2 * j, rdy), ("g", 2 * j + 1, rdy)]
                else:
                    kind = "qkv"[(j - 4) // 2]
                    chunks = [(kind, 4 * ((j - 4) % 2) + q4, rdy) for q4 in range(4)]
                last = project(pG, c0t, chunks, lambda n, k, slot=slot: WA[slot][:, k, n * 128:(n + 1) * 128], fw)
                fw = []
                stA.release(idx, [last])
            posA += 10
            xnT_free[0] = [last]
        if do_mixer:
            t_ag = R.cc("AllGather", ALU.bypass, xn2l.ap().opt(), xn2a.ap().opt(), waits=xn2_store)
        R.barrier()

        if do_mixer:
            wsT = arena_b[:, o1:o1 + 16 * 640].rearrange("p (k n) -> p k n", n=640)
            wsem = R.sem("wsem")
            t_ws = R.dma('sp', wsT, wis_b.ap().rearrange("(k p) n -> p k n", p=128), wsem)
            NTS = SS // 512
            xa = xn2a.ap()

            def mk_xs(i, slot):
                r, ii = divmod(i, SL // 512)
                return [(WA[slot][:, :, :], xa[r * D:(r + 1) * D, ii * 512:(ii + 1) * 512].rearrange("(k p) t -> p k t", p=128))]
            stX = Stream(R, "xsm", 2, NTS, mk_xs)
            for i in range(NTS):
                slot, rdy = stX.get(i)
                cur_xnT[0] = WA[slot]
                chunks = [(kd, 0, rdy) for kd in ("a", "g", "q", "k", "v")]
                last = project(sG, i * 512, chunks, lambda n, k: wsT[:, k, n * 128:(n + 1) * 128], [t_ws])
                stX.release(i, [last])
            cur_xnT[0] = xnT
            R.barrier()

            AFO = 0
            strip = [arena_f[:, 0:1152], arena_f[:, 1152:2304]]
            hk = arena_f[:, 2304:3456]
            oc = arena_f[:, 3456:5504].rearrange("p (a t) -> p a t", t=512)
            tmp = [arena_f[:, 5504 + i * 512:5504 + (i + 1) * 512] for i in range(6)]
            ubuf = [arena_f[:, 8576:8576 + 544], arena_f[:, 9120:9120 + 544]]
            acc = [arena_f[:, 9664:10176], arena_f[:, 10176:10688]]
            frow = arena_f[0:8, 0:FLEN]
            ABO = 0
            qb = [arena_b[:, 32768:33280], arena_b[:, 33280:33792]]
            PT = [arena_b[:, 33792:34816], arena_b[:, 34816:35840]]
            mst = [arena_b[:, 35840:36352], arena_b[:, 36352:36864]]
            stat_sb = sb("stat_sb", [1, 1024], F32)
            stbc = sb("stbc", [128, 1024], F32)
            subg_s = small[:, 44:45]
            R.op('dve', lambda e: e.tensor_scalar(out=subg_s, in0=subg_sb[:, :], scalar1=1.0 - LAM_INIT, scalar2=None, op0=ALU.mult))

            msem = {k: R.sem("m_" + k) for k in ("u0", "u1", "y", "st", "k0", "k1", "v0", "v1", "q0", "q1", "hk", "ms0", "ms1", "f", "bc", "yl0", "yl1", "y0", "y1")}

            ms_free = [None, None]

            def conv_pass1(G, gi, out, lo=0, hi=8):
                S, C = G["S"], G["C"]
                ub_free = [None, None]
                acc_free = [[], []]
                n = 0
                st_tk = []
                stat_free = [None]
                for i in range(S // 512):
                    tlast = None
                    for c in range(C):
                        pc = c if gi == "p" else 8
                        s = n % 2
                        n += 1
                        tl_u = R.dma('sp', ubuf[s][:, 0:542], G["uT"][c * 128:(c + 1) * 128, PADC + i * 512 - 15:PADC + i * 512 + 527], msem["u%d" % s], waits=[ub_free[s]])
                        t = R.op('dve', lambda e, s=s, pc=pc: e.tensor_scalar(out=acc[s], in0=ubuf[s][:, 0:512], scalar1=cpar[:, pc, 0:1], scalar2=cpar[:, pc, CW:CW + 1], op0=ALU.mult, op1=ALU.add),
                                 waits=[tl_u] + acc_free[s])
                        yield
                        for j in range(1, CW):
                            t = R.op('dve', lambda e, s=s, pc=pc, j=j: e.scalar_tensor_tensor(out=acc[s], in0=ubuf[s][:, j:j + 512], scalar=cpar[:, pc, j:j + 1], in1=acc[s], op0=ALU.mult, op1=ALU.add), waits=[t])
                            if j % 2 == 0:
                                yield
                        ty = R.dma('pool', G["yT"][c * 128:(c + 1) * 128, i * 512:(i + 1) * 512], acc[s], msem["y%d" % s], waits=[t])
                        tsq = R.op('pool', lambda e, s=s: e.tensor_tensor(out=ubuf[s][:, 0:512], in0=acc[s], in1=acc[s], op=ALU.mult), waits=[t])
                        b1, w1 = alloc_banks(2, lo, hi)
                        tm1 = R.op('pe', lambda e, s=s, b1=b1: e.matmul(bank(b1), lhsT=onesf[:, :], rhs=acc[s], start=True, stop=True), waits=[t] + w1)
                        tm2 = R.op('pe', lambda e, s=s, b1=b1: e.matmul(bank(b1 + 1), lhsT=onesf[:, :], rhs=ubuf[s][:, 0:512], start=True, stop=True), waits=[tsq])
                        ub_free[s] = tm2
                        acc_free[s] = [ty, tm1, tm2]
                        st_tk.append(ty)
                        if c == 0:
                            t1 = R.op('dve', lambda e, b1=b1: e.tensor_copy(out=stat_sb[0:1, 0:512], in_=bank(b1)[0:1, :]), waits=[tm1, stat_free[0]])
                            t2 = R.op('dve', lambda e, b1=b1: e.tensor_copy(out=stat_sb[0:1, 512:1024], in_=bank(b1 + 1)[0:1, :]), waits=[tm2])
                        else:
                            t1 = R.op('dve', lambda e, b1=b1: e.tensor_tensor(out=stat_sb[0:1, 0:512], in0=stat_sb[0:1, 0:512], in1=bank(b1)[0:1, :], op=ALU.add), waits=[tm1, tlast])
                            t2 = R.op('dve', lambda e, b1=b1: e.tensor_tensor(out=stat_sb[0:1, 512:1024], in0=stat_sb[0:1, 512:1024], in1=bank(b1 + 1)[0:1, :], op=ALU.add), waits=[tm2])
                        tlast = t2
                        free_banks(b1, 2, [t1, t2])
                        yield
                    td = R.dma('pool', G["st"][0:1, i * 512:(i + 1) * 512], stat_sb[0:1, 0:512], msem["st"], waits=[tlast])
                    td = R.dma('pool', G["st"][1:2, i * 512:(i + 1) * 512], stat_sb[0:1, 512:1024], msem["st"], waits=[tlast])
                    stat_free[0] = td
                    st_tk.append(td)
                    yield
                out['tk'] = st_tk

            def conv_pass2(G, gi, stsrc, waits):
                S, C = G["S"], G["C"]
                n = 0
                yl_free = [None, None]
                bc_free = [None]
                fw = list(waits)
                for i in range(S // 512):
                    tb = R.dma('sp', stbc[:, :].rearrange("p (a t) -> p a t", t=512),
                               bass.AP(stsrc, i * 512, [[0, 128], [S, 2], [1, 512]]), msem["bc"], waits=fw + [bc_free[0]])
                    fw = []
                    mean, var = stbc[:, 0:512], stbc[:, 512:1024]
                    t = R.op('dve', lambda e: e.tensor_scalar(out=mean, in0=mean, scalar1=1.0 / CONV, scalar2=None, op0=ALU.mult), waits=[tb])
                    t = R.op('dve', lambda e: e.tensor_tensor(out=tmp[0], in0=mean, in1=mean, op=ALU.mult), waits=[t])
                    t = R.op('dve', lambda e: e.scalar_tensor_tensor(out=var, in0=var, scalar=1.0 / CONV, in1=tmp[0], op0=ALU.mult, op1=ALU.subtract), waits=[t])
                    t = R.op('act', lambda e: e.activation(out=var, in_=var, func=AF.Sqrt, bias=epsl, scale=1.0), waits=[t])
                    t = R.op('dve', lambda e: e.reciprocal(out=var, in_=var), waits=[t])
                    trs = t
                    yield
                    for c in range(C):
                        pc = c if gi == "p" else 8
                        s = n % 2
                        n += 1
                        tl_y = R.dma('sp', acc[s], G["yT"][c * 128:(c + 1) * 128, i * 512:(i + 1) * 512], msem["yl%d" % s], waits=[yl_free[s]])
                        t = R.op('dve', lambda e, s=s: e.tensor_tensor(out=acc[s], in0=acc[s], in1=mean, op=ALU.subtract), waits=[tl_y, trs])
                        t = R.op('dve', lambda e, s=s: e.tensor_tensor(out=acc[s], in0=acc[s], in1=var, op=ALU.mult), waits=[t])
                        ta = R.op('act', lambda e, s=s, pc=pc: e.activation(out=mst[s], in_=acc[s], func=AF.Silu, bias=cpar[:, pc, CW + 2:CW + 3], scale=cpar[:, pc, CW + 1:CW + 2]), waits=[t, ms_free[s]])
                        yl_free[s] = ta
                        td = R.dma('pool', G["mixT"][c * 128:(c + 1) * 128, i * 512:(i + 1) * 512], mst[s], msem["ms%d" % s], waits=[ta])
                        ms_free[s] = td
                        yield
                    bc_free[0] = ta

            def attention(G, gi, bg=None):
                S, C, H = G["S"], G["C"], G["H"]
                NKB, NQC = S // 128, S // 512
                nsl = 2 if H > 1 else 1
                assert 2 * nsl * S <= 32768
                kTb = [arena_b[:, i * S:(i + 1) * S] for i in range(nsl)] * 2
                Vb = [arena_b[:, (nsl + i) * S:(nsl + i + 1) * S] for i in range(nsl)] * 2
                rcol = (lambda h: relb_sb[:, h:h + 1]) if gi == "p" else (lambda h: relb_sb[:, NH:NH + 1])
                lhs_rb = relb_sb[:, 0:NH] if gi == "p" else relb_sb[:, NH:NH + 1]
                tf = None
                for j in range(0, FLEN, 512):
                    wdt = min(512, FLEN - j)
                    bi, w = alloc_banks(1, 0, 4)
                    t = R.op('pe', lambda e, j=j, wdt=wdt, bi=bi: e.matmul(bank(bi)[0:H, 0:wdt], lhsT=lhs_rb, rhs=oh_sb[:, j:j + wdt], start=True, stop=True), waits=w)
                    t = R.op('dve', lambda e, j=j, wdt=wdt, bi=bi: e.tensor_copy(out=frow[0:H, j:j + wdt], in_=bank(bi)[0:H, 0:wdt]), waits=[t])
                    free_banks(bi, 1, [t])
                    tf = t
                t_F = R.dma('pool', G["F"].ap(), frow[0:H, :], msem["f"], waits=[tf])
                kv_free = [None, None]
                q_free = [None, None]
                pt_free = [None, None]
                strip_free = [None, None]
                hk_free = [t_F]
                acc_free_t = []
                oc_free = []
                qn = 0
                un = 0
                for h in range(H):
                    hs = h % 2
                    tk = R.dma('sp', kTb[hs][:, 0:S], G["kT"][h * 128:(h + 1) * 128, :], msem["k%d" % hs], waits=[kv_free[hs]])
                    tv = R.dma('sp', Vb[hs][:, 0:S].rearrange("p (j d) -> p j d", d=128), G["V"][h].rearrange("(j p) d -> p j d", p=128), msem["v%d" % hs], waits=[kv_free[hs]])
                    th = R.dma('sp', hk, bass.AP(G["F"], h * FLEN, [[1, 128], [1, 1152]]), msem["hk"], waits=hk_free + [t_F])
                    tsp = []
                    for j in range(3):
                        bi, w = alloc_banks(1, 0, 4)
                        t = R.op('pe', lambda e, j=j, bi=bi: e.matmul(bank(bi)[:, 0:384], lhsT=jflip[:, :], rhs=hk[:, j * 384:(j + 1) * 384], start=True, stop=True), waits=[th] + w)
                        tpe = t
                        t = R.op('dve', lambda e, j=j, bi=bi, hs=hs: e.tensor_copy(out=strip[hs][:, j * 384:(j + 1) * 384], in_=bank(bi)[:, 0:384]), waits=[t, strip_free[hs]])
                        free_banks(bi, 1, [t])
                        tsp.append(t)
                    hk_free = [tpe]
                    V3o = Vb[hs][:, 0:S].rearrange("p (j d) -> p j d", d=128)
                    last_pv = None
                    for qc in range(NQC):
                        qs = qn % 2
                        qn += 1
                        tq = R.dma('sp', qb[qs], G["qT"][h * 128:(h + 1) * 128, qc * 512:(qc + 1) * 512], msem["q%d" % qs], waits=[q_free[qs]])
                        pend = None

                        def pv_unit(pend, first, lastu):
                            ps_, ta_, kb_ = pend
                            V3 = V3o
                            wt = [ta_] + (acc_free_t if first else [])
                            R.op('pe', lambda e: e.matmul(bank(4), lhsT=V3[:, kb_, :], rhs=PT[ps_][:, 0:512], start=first, stop=lastu), waits=wt, sig=False)
                            R.op('pe', lambda e: e.matmul(bank(5), lhsT=V3[:, kb_, :], rhs=PT[ps_][:, 512:1024], start=first, stop=lastu), sig=False)
                            R.op('pe', lambda e: e.matmul(bank(6), lhsT=onesb[:, :], rhs=PT[ps_][:, 0:512], start=first, stop=lastu), sig=False)
                            r = R.op('pe', lambda e: e.matmul(bank(7), lhsT=onesb[:, :], rhs=PT[ps_][:, 512:1024], start=first, stop=lastu))
                            pt_free[ps_] = r
                            return r
                        for kb in range(NKB):
                            e_ = kb - 4 * qc
                            near = (-1 <= e_ <= 4)
                            bi, w = alloc_banks(2, 0, 4)
                            wt = w + ([tk, tv, tq] if kb == 0 else [])
                            R.op('pe', lambda e, bi=bi, kb=kb, hs=hs, qs=qs: e.matmul(bank(bi), lhsT=kTb[hs][0:64, kb * 128:(kb + 1) * 128], rhs=qb[qs][0:64, :], start=True, stop=True), waits=wt, sig=False)
                            ts_ = R.op('pe', lambda e, bi=bi, kb=kb, hs=hs, qs=qs: e.matmul(bank(bi + 1), lhsT=kTb[hs][64:128, kb * 128:(kb + 1) * 128], rhs=qb[qs][64:128, :], start=True, stop=True))
                            if near:
                                off = 512 - 128 * e_
                                R.op('dve', lambda e, bi=bi, hs=hs, off=off: e.tensor_tensor(out=bank(bi), in0=bank(bi), in1=strip[hs][:, off:off + 512], op=ALU.add), waits=[ts_] + tsp)
                                ts_ = R.op('dve', lambda e, bi=bi, hs=hs, off=off: e.tensor_tensor(out=bank(bi + 1), in0=bank(bi + 1), in1=strip[hs][:, off:off + 512], op=ALU.add))
                                bias_ap = 0.0
                            else:
                                bias_ap = strip[hs][:, 1151:1152] if e_ < 0 else strip[hs][:, 0:1]
                            ps = un % 2
                            un += 1
                            ta = R.op('act', lambda e, bi=bi, ps=ps, bias_ap=bias_ap: e.activation(out=PT[ps], in_=bank(bi, 2), func=AF.Exp, bias=bias_ap, scale=1.0),
                                      waits=[ts_, pt_free[ps]] + tsp)
                            free_banks(bi, 2, [ta])
                            if pend is not None:
                                last_pv = pv_unit(pend, pend[2] == 0, False)
                            pend = (ps, ta, kb)
                            if bg is not None:
                                next(bg, None)
                        last_pv = pv_unit(pend, pend[2] == 0, True)
                        q_free[qs] = last_pv
                        t0_ = R.op('dve', lambda e: e.tensor_copy(out=oc[:, 0, :], in_=bank(4)), waits=[last_pv] + oc_free)
                        t1_ = R.op('act', lambda e: e.activation(out=oc[:, 1, :], in_=bank(5), func=AF.Copy), waits=[last_pv] + oc_free)
                        t2_ = R.op('dve', lambda e: e.tensor_copy(out=oc[:, 2, :], in_=bank(6)))
                        t3_ = R.op('act', lambda e: e.activation(out=oc[:, 3, :], in_=bank(7), func=AF.Copy))
                        t = R.op('dve', lambda e: e.reciprocal(out=oc[:, 2, :], in_=oc[:, 2, :]), waits=[t2_])
                        t = R.op('dve', lambda e: e.reciprocal(out=oc[:, 3, :], in_=oc[:, 3, :]), waits=[t3_])
                        t = R.op('dve', lambda e: e.tensor_tensor(out=oc[:, 0, :], in0=oc[:, 0, :], in1=oc[:, 2, :], op=ALU.mult), waits=[t0_, t])
                        t = R.op('dve', lambda e: e.tensor_tensor(out=oc[:, 1, :], in0=oc[:, 1, :], in1=oc[:, 3, :], op=ALU.mult), waits=[t1_, t])
                        t = R.op('dve', lambda e: e.scalar_tensor_tensor(out=oc[:, 0, :], in0=oc[:, 1, :], scalar=neg_lam, in1=oc[:, 0, :], op0=ALU.mult, op1=ALU.add), waits=[t, t_lam])
                        tsq = R.op('dve', lambda e: e.tensor_tensor(out=oc[:, 1, :], in0=oc[:, 0, :], in1=oc[:, 0, :], op=ALU.mult), waits=[t])
                        bi, w = alloc_banks(1, 0, 4)
                        tm = R.op('pe', lambda e, bi=bi: e.matmul(bank(bi), lhsT=onesf[:, :], rhs=oc[:, 1, :], start=True, stop=True), waits=[tsq] + w)
                        t = R.op('act', lambda e, bi=bi: e.activation(out=oc[:, 2, :], in_=bank(bi), func=AF.Sqrt, bias=epsl, scale=1.0 / 128), waits=[tm])
                        free_banks(bi, 1, [t])
                        t = R.op('dve', lambda e: e.reciprocal(out=oc[:, 2, :], in_=oc[:, 2, :]), waits=[t])
                        ms = qn % 2
                        t = R.op('dve', lambda e, ms=ms: e.scalar_tensor_tensor(out=mst[ms], in0=oc[:, 0, :], scalar=subg_s, in1=oc[:, 2, :], op0=ALU.mult, op1=ALU.mult), waits=[t, ms_free[ms]])
                        acc_free_t = [t0_, t1_, t2_, t3_]
                        oc_free = [t]
                        ms_free[ms] = R.dma('pool', G["mixT"][(C + h) * 128:(C + h + 1) * 128, qc * 512:(qc + 1) * 512], mst[ms], msem["ms%d" % ms], waits=[t])
                    kv_free[hs] = last_pv
                    strip_free[hs] = ta

            def bg_chain():
                o1 = {}
                yield from conv_pass1(sG, "s", o1, 0, 4)
                t_ar = R.cc("AllReduce", ALU.add, sG["st"].ap().opt(), st_all.ap().opt(), waits=o1['tk'])
                yield
                yield from conv_pass2(sG, "s", st_all, [t_ar] + o1['tk'])
                o2 = {}
                yield from conv_pass1(pG, "p", o2, 0, 4)
                yield from conv_pass2(pG, "p", pG["st"], o2['tk'])

            rs_sem = [None]
            for gi in ("s", "p"):
                G = grp[gi]
                if gi == "s":
                    bg = bg_chain()
                    attention(G, gi, bg)
                    for _ in bg:
                        pass
                else:
                    attention(G, gi)
                R.barrier(exclude=rs_sem[0])
                if gi == "s":
                    wosT = arena_b[:, 0:4096].rearrange("p (k n) -> p k n", n=D)
                    t_wo = R.dma('sp', wosT, wos_b.ap().rearrange("(k p) n -> p k n", p=128), wsem)
                    mxs = [arena_b[:, 4096:5120].rearrange("p (k t) -> p k t", t=512), arena_b[:, 5120:6144].rearrange("p (k t) -> p k t", t=512)]
                    mx_free = [None, None]
                    pst = [arena_f[:, 0:512], arena_f[:, 512:1024], arena_f[:, 1024:1536], arena_f[:, 1536:2048]]
                    psem = [R.sem("ps%d" % i) for i in range(4)]
                    p_free = [None] * 4
                    pn = 0
                    pouts = []
                    for i in range(SS // 512):
                        s = i % 2
                        tm_ = R.dma('sp', mxs[s], G["mixT"].ap().rearrange("(k p) t -> p k t", p=128)[:, :, i * 512:(i + 1) * 512], msem["q%d" % s], waits=[mx_free[s]])
                        lastp = None
                        for b in range(4):
                            for s4 in range(4):
                                bi, w = alloc_banks(1)
                                R.op('pe', lambda e, s=s, b=b, s4=s4, bi=bi: e.matmul(bank(bi), lhsT=mxs[s][:, 0, b * 128:(b + 1) * 128], rhs=wosT[:, 0, s4 * 512:(s4 + 1) * 512], start=True, stop=False), waits=[tm_, t_wo] + w, sig=False)
                                r = R.op('pe', lambda e, s=s, b=b, s4=s4, bi=bi: e.matmul(bank(bi), lhsT=mxs[s][:, 1, b * 128:(b + 1) * 128], rhs=wosT[:, 1, s4 * 512:(s4 + 1) * 512], start=False, stop=True))
                                lastp = r
                                pi = pn % 4
                                pn += 1
                                if pn % 2:
                                    t = R.op('dve', lambda e, pi=pi, bi=bi: e.tensor_copy(out=pst[pi], in_=bank(bi)), waits=[r, p_free[pi]])
                                else:
                                    t = R.op('act', lambda e, pi=pi, bi=bi: e.activation(out=pst[pi], in_=bank(bi), func=AF.Copy), waits=[r, p_free[pi]])
                                free_banks(bi, 1, [t])
                                p_free[pi] = R.dma('pool', part[i * 512 + b * 128:i * 512 + (b + 1) * 128, s4 * 512:(s4 + 1) * 512], pst[pi], psem[pi], waits=[t])
                                pouts.append(p_free[pi])
                        mx_free[s] = lastp
                    t_rs = R.cc("ReduceScatter", ALU.add, part.ap().opt(), rso.ap().opt(), waits=pouts[-8:] + [(q, q.v) for q in psem])
                    rs_sem[0] = t_rs[0]
                    R.barrier(exclude=rs_sem[0])

        tilesC = [t_ for t_ in tiles if t_[0] == "p"] + [t_ for t_ in tiles if t_[0] == "s"]
        R.barrier()
        itemsC = []
        for ti, tl_ in enumerate(tilesC):
            if tl_[0] == "p" and do_mixer:
                for j in range(4):
                    itemsC.append(("wo", j))
            for j in range(NSLAB1):
                itemsC.append(("w2i", j))

        def mk_slabC(i, slot):
            key, j = itemsC[i]
            wv = wview(key)
            if key == "w2i":
                return [(WA[slot][:, :, 0:256], wv[:, :, j * 256:(j + 1) * 256]),
                        (WA[slot][:, :, 256:512], wv[:, :, DFF + j * 256:DFF + (j + 1) * 256])]
            return [(WA[slot][:, :, :], wv[:, :, j * 512:(j + 1) * 512])]
        stC = Stream(R, "wc", 2, len(itemsC), mk_slabC)
        stW2c = Stream(R, "w2c", 2, NTILE * 4 * KG, mk_w2("w2o"))
        posC = 0
        ysem = [R.sem("ysem%d" % b) for b in range(4)]
        rsem = [R.sem("rs0"), R.sem("rs1")]
        r_free = [None, None]
        rn = 0
        mxsem = R.sem("mxsem")
        for ti, tl_ in enumerate(tilesC):
            t0 = tok0(tl_)
            for b in range(4):
                xt_ready[b] = R.dma('sp', xt[:, b, :], x1d[t0 + b * 128:t0 + (b + 1) * 128, :], xsem[b], waits=xt_free[b])
                xt_free[b] = []
            x2t = list(xt_ready)
            if do_mixer and tl_[0] == "s":
                i = tl_[1]
                for b in range(4):
                    for s4 in range(4):
                        s = rn % 2
                        rn += 1
                        tr = R.dma('sp', sg[s], rso[i * 512 + b * 128:i * 512 + (b + 1) * 128, s4 * 512:(s4 + 1) * 512], rsem[s], waits=[r_free[s], sg_free[s]])
                        t = R.op('dve', lambda e, b=b, s4=s4, s=s: e.tensor_tensor(out=xt[:, b, s4 * 512:(s4 + 1) * 512], in0=xt[:, b, s4 * 512:(s4 + 1) * 512], in1=sg[s], op=ALU.add), waits=[tr, xt_ready[b]])
                        r_free[s] = t
                        sg_free[s] = t
                        x2t[b] = t
            elif do_mixer:
                i = tl_[1]
                tmx = R.dma('sp', xnT[:, :, :], pG["mixT"].ap().rearrange("(k p) t -> p k t", p=128)[:, :, i * 512:(i + 1) * 512], mxsem, waits=xnT_free[0] + hT_free[0])
                lastmm = None
                for s4 in range(4):
                    idx = posC + s4
                    slot, rdy = stC.get(idx)
                    bis = []
                    ws = []
                    for b in range(4):
                        bi, w = alloc_banks(1)
                        bis.append(bi)
                        ws += w
                    for k in range(KC):
                        for b in range(4):
                            lastk = (k == KC - 1)
                            wt = ([rdy, tmx] + ws) if (k == 0 and b == 0) else []
                            r = R.op('pe', lambda e, slot=slot, k=k, b=b, bi=bis[b], lastk=lastk: e.matmul(bank(bi), lhsT=xnT[:, k, b * 128:(b + 1) * 128], rhs=WA[slot][:, k, :], start=(k == 0), stop=lastk), waits=wt, sig=lastk)
                            if lastk:
                                lastmm = r
                                td = R.op('dve', lambda e, b=b, s4=s4, bi=bis[b]: e.tensor_tensor(out=xt[:, b, s4 * 512:(s4 + 1) * 512], in0=bank(bi), in1=xt[:, b, s4 * 512:(s4 + 1) * 512], op=ALU.add), waits=[r, xt_ready[b]])
                                free_banks(bis[b], 1, [td])
                                x2t[b] = td
                    stC.release(idx, [lastmm])
                posC += 4
                xnT_free[0] = [lastmm]
            xr = rmsnorm_tile(x2t)
            x3t = ffn(xr, stC, posC, stW2c, ti * 4 * KG, ti)
            posC += NSLAB1
            base = (ss_i[0] % 4) * 4
            ss_i[0] += 1
            t = None
            for b in range(4):
                t = R.op('act', lambda e, b=b, base=base: e.activation(out=xnb[b], in_=xt[:, b, :], func=AF.Square, accum_out=ss[:, base + b:base + b + 1]), waits=[x3t[b], xnb_free[b]])
                xnb_free[b] = t
            t = R.op('act', lambda e, base=base: e.activation(out=ss[:, 16 + base:20 + base], in_=ss[:, base:base + 4], func=AF.Sqrt, bias=epsr, scale=1.0 / D), waits=[t])
            t_r = R.op('dve', lambda e, base=base: e.reciprocal(out=ss[:, 16 + base:20 + base], in_=ss[:, 16 + base:20 + base]), waits=[t])
            for b in range(4):
                t = R.op('dve', lambda e, b=b, base=base: e.scalar_tensor_tensor(out=xt[:, b, :], in0=xt[:, b, :], scalar=ss[:, 16 + base + b:17 + base + b], in1=gfin_bc[:, :], op0=ALU.mult, op1=ALU.mult), waits=[t_r, t_const])
                td = R.dma('pool', y[t0 + b * 128:t0 + (b + 1) * 128, :], xt[:, b, :], ysem[b], waits=[t])
                xt_free[b] = [td]
        R.barrier()

        with nc.Block() as block:
            @block.sync
            def _(e):
                for f in R.q['sp']:
                    f(e)

            @block.tensor
            def _(e):
                for f in R.q['pe']:
                    f(e)

            @block.scalar
            def _(e):
                for f in R.q['act']:
                    f(e)

            @block.vector
            def _(e):
                for f in R.q['dve']:
                    f(e)

            @block.gpsimd
            def _(e):
                for f in R.q['pool']:
                    f(e)
    return nc


_NC_CACHE = {}


def kernel(**inp):
    f32 = lambda a: np.ascontiguousarray(np.asarray(a, dtype=np.float32))
    xp = f32(inp["x_prompt"])
    xsm = f32(inp["x_sample"])
    B, SP, _ = xp.shape
    SS = xsm.shape[1]
    SL = SS // NCORES
    DFF = inp["ffn1_w_out"].shape[1]
    assert B == NCORES and xsm.shape[0] == 1
    key = (SP, SS, DFF)
    if key not in _NC_CACHE:
        _NC_CACHE[key] = build_nc(SP, SS, DFF)
    nc = _NC_CACHE[key]
    w1i, w1o = f32(inp["ffn1_w_in"])[0], f32(inp["ffn1_w_out"])[0]
    w2i, w2o = f32(inp["ffn2_w_in"])[0], f32(inp["ffn2_w_out"])[0]
    wi, wo = f32(inp["w_in"])[0], f32(inp["w_out"])[0]
    g1, gm, g2 = f32(inp["ffn1_norm"])[0], f32(inp["mix_norm"])[0], f32(inp["ffn2_norm"])[0]
    convw = f32(inp["conv_w"])[0].T.copy()
    convb, lng, lnb = f32(inp["conv_b"])[0], f32(inp["conv_ln_g"])[0], f32(inp["conv_ln_b"])[0]
    relb = f32(inp["rel_bias"])
    lam4 = np.stack([f32(inp["lambda_q1"])[0], f32(inp["lambda_k1"])[0], f32(inp["lambda_q2"])[0], f32(inp["lambda_k2"])[0]], axis=1).copy()
    ii = np.arange(FLEN)
    bk = rel_bucket_np(639 - ii)
    ohrev = (bk[None, :] == np.arange(32)[:, None]).astype(np.float32)
    identf = np.eye(128, dtype=np.float32)
    jflip = np.ascontiguousarray(identf[::-1])
    in_maps = []
    for c in range(NCORES):
        cs = slice(128 * c, 128 * (c + 1))
        cols = np.concatenate([np.arange(128 * c, 128 * c + 128), 1024 + np.arange(128 * c, 128 * c + 128),
                               2048 + np.arange(128 * c, 128 * c + 128), 3072 + np.arange(128 * c, 128 * c + 128),
                               4096 + np.arange(128 * c, 128 * c + 128)])
        rs = slice(256 * c, 256 * (c + 1))
        fr = slice((DFF // 8) * c, (DFF // 8) * (c + 1))
        m = {
            "xs": np.concatenate([xp[c], xsm[0, SL * c:SL * (c + 1)]], axis=0),
            "w1i": w1i[rs], "w1o": w1o[fr], "wi": wi[rs], "wo": wo[rs], "w2i": w2i[rs], "w2o": w2o[fr],
            "wis": wi[:, cols], "wos": np.concatenate([wo[cs], wo[1024 + 128 * c:1024 + 128 * (c + 1)]], axis=0),
            "g1": g1[rs, None], "gm": gm[rs, None], "g2": g2[rs, None], "gmf": gm[:, None],
            "gfin": f32(inp["final_norm"])[None, :],
            "convw": convw, "convb": convb[:, None], "lng": lng[:, None], "lnb": lnb[:, None],
            "convw_s": convw[cs], "convb_s": convb[cs, None], "lng_s": lng[cs, None], "lnb_s": lnb[cs, None],
            "lam4": lam4, "subg": f32(inp["subln_g"])[0][:, None],
            "relb": relb, "relb_s": relb[:, c:c + 1],
            "ohrev": ohrev, "identf": identf, "jflip": jflip,
        }
        in_maps.append({k: np.ascontiguousarray(v, dtype=np.float32) for k, v in m.items()})
    res = run_bass_kernel_spmd(nc, in_maps, core_ids=list(range(NCORES)))
    ys = [np.asarray(r["y"], dtype=np.float32) for r in res.results]
    y_prompt = np.stack([yy[:SP] for yy in ys], axis=0)
    y_sample = np.concatenate([yy[SP:] for yy in ys], axis=0)[None]
    return (y_prompt, y_sample)
```
